# Optimizing a Trainium2 kernel written in Bass

```python
import jax, jax.numpy as jnp
from jax import lax
import numpy as np

D_MODEL = 2048
BATCH = 2
SEQ = 4096
DEPTH = 4

CTX_LEN = 256
GRID_W = 64
MIX_WIDTH = D_MODEL
M_WIDTH = MIX_WIDTH // 2
R_WIDTH = MIX_WIDTH - M_WIDTH
M_HEADS = 4
M_DV = M_WIDTH // M_HEADS
M_DQK = M_DV // 2
M_QK = M_HEADS * M_DQK
M_CHUNK = 64
R_BLOCKS = 16
R_BLOCK = R_WIDTH // R_BLOCKS
CONV_W = 4
CONV_PAD = (1, 2)
LRU_C = 8.0
EPS = 1e-6
IN_SIZES = (M_QK, M_QK, M_WIDTH, M_WIDTH, M_WIDTH, 4 * M_HEADS, R_WIDTH, R_WIDTH)
IN_WIDTH = sum(IN_SIZES)
IN_SPLITS = [int(s) for s in np.cumsum(IN_SIZES)[:-1]]

kernel_name = 'hymba_mlstm_rglru_prefix_dit'


def rmsnorm(t, g):
    tf = t.astype(jnp.float32)
    y = tf * lax.rsqrt(jnp.mean(tf * tf, axis=-1, keepdims=True) + EPS)
    return (y * g.astype(jnp.float32)).astype(t.dtype)


def dwconv(t, w, b):
    y = lax.conv_general_dilated(t, w[:, None, :].astype(t.dtype), (1,), [CONV_PAD],
                                 dimension_numbers=('NWC', 'WIO', 'NWC'),
                                 feature_group_count=t.shape[-1])
    return y + b.astype(t.dtype)


def to_col_major(t):
    b, n, ch = t.shape
    rows = n // GRID_W
    return t.reshape(b, rows, GRID_W, ch).transpose(0, 2, 1, 3).reshape(b, n, ch)


def to_row_major(t):
    b, n, ch = t.shape
    rows = n // GRID_W
    return t.reshape(b, GRID_W, rows, ch).transpose(0, 2, 1, 3).reshape(b, n, ch)


def mlstm_chunkwise(q, k, v, ig, lf, state):
    bsz, nh, t, _ = q.shape
    nc = t // M_CHUNK

    def chunks(a):
        a = a.reshape(a.shape[:2] + (nc, M_CHUNK) + a.shape[3:])
        return jnp.moveaxis(a, 2, 0)

    causal = jnp.tril(jnp.ones((M_CHUNK, M_CHUNK), dtype=bool))

    def step(carry, inp):
        c_st, n_st, m_st = carry
        qc, kc, vc, ic, fc = inp
        b = jnp.cumsum(fc, axis=-1)
        dmat = jnp.where(causal, b[..., :, None] - b[..., None, :] + ic[..., None, :], -jnp.inf)
        g = b + m_st[..., None]
        m_t = jnp.maximum(g, jnp.max(dmat, axis=-1))
        inter = jnp.exp(g - m_t)
        s = jnp.einsum('bhtd,bhsd->bhts', qc, kc) * jnp.exp(dmat - m_t[..., None])
        num = inter[..., None] * jnp.einsum('bhvd,bhtd->bhtv', c_st, qc) + jnp.einsum('bhts,bhsv->bhtv', s, vc)
        nq = inter * jnp.einsum('bhd,bhtd->bht', n_st, qc) + jnp.sum(s, axis=-1)
        h = num / jnp.maximum(jnp.abs(nq), jnp.exp(-m_t))[..., None]
        m_new = m_t[..., -1]
        w_s = jnp.exp(b[..., -1:] - b + ic - m_new[..., None])
        decay = jnp.exp(b[..., -1] + m_st - m_new)
        c_new = decay[..., None, None] * c_st + jnp.einsum('bhs,bhsv,bhsd->bhvd', w_s, vc, kc)
        n_new = decay[..., None] * n_st + jnp.einsum('bhs,bhsd->bhd', w_s, kc)
        return (c_new, n_new, m_new), h

    state, hs = lax.scan(step, state, (chunks(q), chunks(k), chunks(v), chunks(ig), chunks(lf)))
    hs = jnp.moveaxis(hs, 0, 2).reshape(bsz, nh, t, v.shape[-1])
    return hs, state


def mlstm_zero_state(bsz):
    f32 = jnp.float32
    return (jnp.zeros((bsz, M_HEADS, M_DV, M_DQK), f32), jnp.zeros((bsz, M_HEADS, M_DQK), f32),
            jnp.zeros((bsz, M_HEADS), f32))


def mlstm_inputs(qk_raw, v, gates, conv_w, conv_b, b_gate):
    f32 = jnp.float32
    qk = jax.nn.silu(dwconv(qk_raw, conv_w, conv_b)).astype(f32)
    q, k = jnp.split(qk, 2, axis=-1)

    def heads(a, dh):
        return a.reshape(a.shape[0], a.shape[1], M_HEADS, dh).transpose(0, 2, 1, 3)

    q = heads(q, M_DQK)
    k = heads(k, M_DQK) * (M_DQK ** -0.5)
    vh = heads(v.astype(f32), M_DV)
    gt = (gates.astype(f32) + b_gate.astype(f32)).transpose(0, 2, 1)
    i_f, f_f, i_b, f_b = jnp.split(gt, 4, axis=1)
    return q, k, vh, (i_f, jax.nn.log_sigmoid(f_f)), (i_b, jax.nn.log_sigmoid(f_b))


def mlstm_bidir(q, k, v, gf, gb, state_f, state_b):
    h_f, s_f = mlstm_chunkwise(q, k, v, gf[0], gf[1], state_f)
    fl = lambda a: jnp.flip(a, axis=2)
    h_b, s_b = mlstm_chunkwise(fl(q), fl(k), fl(v), fl(gb[0]), fl(gb[1]), state_b)
    return h_f + fl(h_b), s_f, s_b


def heads_to_seq(h):
    b, nh, t, dv = h.shape
    return h.transpose(0, 2, 1, 3).reshape(b, t, nh * dv)


def mlstm_out(hseq, o, z, g):
    f32 = jnp.float32
    b, t, _ = hseq.shape
    hh = hseq.reshape(b, t, M_HEADS, M_DV)
    hh = hh * lax.rsqrt(jnp.mean(hh * hh, axis=-1, keepdims=True) + EPS)
    hn = hh.reshape(b, t, M_WIDTH) * g.astype(f32)
    return hn * jax.nn.sigmoid(o.astype(f32)) * jax.nn.silu(z.astype(f32))


def _lin_comb(left, right):
    a_l, u_l = left
    a_r, u_r = right
    return a_l * a_r, a_r * u_l + u_r


def rglru(xc, w_r, b_r, w_i, b_i, lam, h0):
    bsz, t, ch = xc.shape
    xb = xc.reshape(bsz, t, R_BLOCKS, R_BLOCK)
    r = jax.nn.sigmoid(jnp.einsum('btgi,gij->btgj', xb, w_r).reshape(bsz, t, ch) + b_r)
    ii = jax.nn.sigmoid(jnp.einsum('btgi,gij->btgj', xb, w_i).reshape(bsz, t, ch) + b_i)
    log_a = -LRU_C * r * jax.nn.softplus(-lam)
    a = jnp.exp(log_a)
    u = jnp.sqrt(-jnp.expm1(2.0 * log_a)) * (ii * xc)
    a_cum, h = lax.associative_scan(_lin_comb, (a, u), axis=1)
    h = h + a_cum * h0[:, None, :]
    return h, h[:, -1]


def rglru_branch(xr_lat, xr_ctx, conv_w, conv_b, w_rg, b_rg, lam):
    f32 = jnp.float32
    xl = dwconv(xr_lat, conv_w, conv_b).astype(f32)
    xc = dwconv(xr_ctx, conv_w, conv_b).astype(f32)
    w_rg = w_rg.astype(f32)
    b_rg = b_rg.astype(f32)
    lam = lam.astype(f32)
    fwd = (w_rg[0, 0], b_rg[0, 0], w_rg[0, 1], b_rg[0, 1], lam[0])
    bwd = (w_rg[1, 0], b_rg[1, 0], w_rg[1, 1], b_rg[1, 1], lam[1])
    zeros = jnp.zeros((xc.shape[0], R_WIDTH), f32)
    h_cf, s_f = rglru(xc, *fwd, zeros)
    h_cb, s_b = rglru(jnp.flip(xc, 1), *bwd, zeros)
    h_lf, _ = rglru(xl, *fwd, s_f)
    h_lb, _ = rglru(jnp.flip(xl, 1), *bwd, s_b)
    return h_lf + jnp.flip(h_lb, 1), h_cf + jnp.flip(h_cb, 1)


def setup_inputs(seed: int = 0) -> dict:
    key = jax.random.key(seed)
    ks = jax.random.split(key, 24)
    f32 = jnp.float32
    L, D, H = DEPTH, D_MODEL, M_HEADS

    def nrm(k, shape, scale=1.0):
        return jax.random.normal(k, shape, f32) * scale

    f_bias = jnp.linspace(3.0, 6.0, H, dtype=f32)
    i_noise = nrm(ks[8], (L, 2, H), 0.01)
    f_noise = nrm(ks[9], (L, 2, H), 0.01)
    b_gate = jnp.concatenate([i_noise[:, 0], f_bias + f_noise[:, 0], i_noise[:, 1], f_bias + f_noise[:, 1]], axis=-1)
    u = jax.random.uniform(ks[15], (L, 2, R_WIDTH), f32, 0.9, 0.999)
    s = u ** (1.0 / LRU_C)
    lru_lambda = jnp.log(s) - jnp.log1p(-s)
    return {
        'x': nrm(ks[0], (BATCH, SEQ, D)),
        'c': nrm(ks[1], (BATCH, D)),
        'ctx': nrm(ks[2], (BATCH, CTX_LEN, D)),
        'c_ctx': nrm(ks[3], (D,)),
        'w_mod': nrm(ks[4], (L, D, 3 * D), D ** -0.5),
        'b_mod': nrm(ks[5], (L, 3 * D), 0.01),
        'norm_g': 1.0 + nrm(ks[6], (L, D), 0.01),
        'w_in': nrm(ks[7], (L, D, IN_WIDTH), D ** -0.5),
        'b_gate': b_gate,
        'conv_qk_w': nrm(ks[10], (L, CONV_W, 2 * M_QK), CONV_W ** -0.5),
        'conv_qk_b': nrm(ks[11], (L, 2 * M_QK), 0.01),
        'm_norm_g': 1.0 + nrm(ks[12], (L, M_WIDTH), 0.01),
        'conv_r_w': nrm(ks[13], (L, CONV_W, R_WIDTH), CONV_W ** -0.5),
        'conv_r_b': nrm(ks[14], (L, R_WIDTH), 0.01),
        'w_rg': nrm(ks[16], (L, 2, 2, R_BLOCKS, R_BLOCK, R_BLOCK), R_BLOCK ** -0.5),
        'b_rg': nrm(ks[17], (L, 2, 2, R_WIDTH), 0.01),
        'lru_lambda': lru_lambda,
        'w_out': nrm(ks[18], (L, MIX_WIDTH, D), MIX_WIDTH ** -0.5),
        'final_g': 1.0 + nrm(ks[19], (D,), 0.01),
    }


def reference(x, c, ctx, c_ctx, w_mod, b_mod, norm_g, w_in, b_gate, conv_qk_w, conv_qk_b, m_norm_g,
              conv_r_w, conv_r_b, w_rg, b_rg, lru_lambda, w_out, final_g):
    bsz = x.shape[0]
    for l in range(DEPTH):
        last = l == DEPTH - 1
        mod_x = jax.nn.silu(c) @ w_mod[l] + b_mod[l]
        mod_c = jax.nn.silu(c_ctx) @ w_mod[l] + b_mod[l]
        sh_x, sc_x, gt_x = jnp.split(mod_x[:, None, :], 3, axis=-1)
        sh_c, sc_c, gt_c = jnp.split(mod_c, 3, axis=-1)
        hx = rmsnorm(x, norm_g[l]) * (1.0 + sc_x) + sh_x
        hc = rmsnorm(ctx, norm_g[l]) * (1.0 + sc_c) + sh_c
        px = jnp.split(hx @ w_in[l], IN_SPLITS, axis=-1)
        pc = jnp.split(hc @ w_in[l], IN_SPLITS, axis=-1)

        qc_, kc_, vc_, gfc, gbc = mlstm_inputs(jnp.concatenate(pc[0:2], axis=-1), pc[2], pc[5],
                                               conv_qk_w[l], conv_qk_b[l], b_gate[l])
        h_c, st_f, st_b = mlstm_bidir(qc_, kc_, vc_, gfc, gbc, mlstm_zero_state(bsz), mlstm_zero_state(bsz))
        ql, kl, vl, gfl, gbl = mlstm_inputs(to_col_major(jnp.concatenate(px[0:2], axis=-1)),
                                            to_col_major(px[2]), to_col_major(px[5]),
                                            conv_qk_w[l], conv_qk_b[l], b_gate[l])
        h_l, _, _ = mlstm_bidir(ql, kl, vl, gfl, gbl, st_f, st_b)
        ym_x = mlstm_out(to_row_major(heads_to_seq(h_l)), px[3], px[4], m_norm_g[l])

        yr_x, yr_c = rglru_branch(px[6], pc[6], conv_r_w[l], conv_r_b[l], w_rg[l], b_rg[l], lru_lambda[l])
        yr_x = yr_x * jax.nn.silu(px[7].astype(jnp.float32))

        yx = jnp.concatenate([ym_x, yr_x], axis=-1).astype(x.dtype)
        x = x + gt_x * (yx @ w_out[l])
        if not last:
            ym_c = mlstm_out(heads_to_seq(h_c), pc[3], pc[4], m_norm_g[l])
            yr_c = yr_c * jax.nn.silu(pc[7].astype(jnp.float32))
            yc = jnp.concatenate([ym_c, yr_c], axis=-1).astype(ctx.dtype)
            ctx = ctx + gt_c * (yc @ w_out[l])
    return rmsnorm(x, final_g)
```

```python
from contextlib import ExitStack
import numpy as np
import ml_dtypes
import concourse.bass as bass
import concourse.mybir as mybir
from concourse.bass_utils import run_bass_kernel_spmd

F32 = mybir.dt.float32
BF16 = mybir.dt.bfloat16
ALU = mybir.AluOpType
AF = mybir.ActivationFunctionType
AX = mybir.AxisListType

L = 4
D = 2048
KC = 16
SEQ = 4096
CTX = 256
T = SEQ + CTX
NT = T // 128
OWN = 1088
EPS = 1e-6
NCORES = 8
PADT = 4360
CT0 = 1
LT0 = 260


class Buf:
    __slots__ = ("w", "r", "excl")

    def __init__(self, excl=False):
        self.w = None
        self.r = {}
        self.excl = excl


def bufs(n, excl=False):
    return [Buf(excl) for _ in range(n)]


class Sched:
    ENG = ("pe", "act", "dve", "pool", "sp")

    def __init__(self, nc, same_engine_sync=True):
        self.nc = nc
        self.same = same_engine_sync
        self.stream = {}
        for s in ("pe", "act", "dve", "pool"):
            self.stream[s] = {"sem": nc.alloc_semaphore("s_" + s), "inc": 1, "n": 0}
        self.dpool = {"sp": 40, "pool": 16, "act": 8}
        self.dnext = {"sp": 0, "pool": 0, "act": 0}
        for q, n in self.dpool.items():
            for i in range(n):
                nm = "%sq%d" % (q, i)
                self.stream[nm] = {"sem": nc.alloc_semaphore("s_" + nm), "inc": 16, "n": 0}
        self.ops = {e: [] for e in self.ENG}
        self.known = {e: {} for e in self.ENG}

    def _waits(self, eng, own, reads, writes):
        need = {}

        def add(sw):
            s, i = sw
            if need.get(s, -1) < i:
                need[s] = i

        for b in reads:
            if b.w is not None:
                add(b.w)
            if b.excl:
                for s, i in b.r.items():
                    if s != own:
                        add((s, i))
        for b in writes:
            if b.w is not None:
                add(b.w)
            for s, i in b.r.items():
                add((s, i))
        waits = []
        kn = self.known[eng]
        for s, i in need.items():
            if s == own and (eng == "pe" or not self.same):
                continue
            if kn.get(s, -1) >= i:
                continue
            kn[s] = i
            st = self.stream[s]
            waits.append((st["sem"], (i + 1) * st["inc"]))
        return waits

    def _commit(self, s, idx, reads, writes):
        for b in reads:
            if b.r.get(s, -1) < idx:
                b.r[s] = idx
        for b in writes:
            b.w = (s, idx)
            b.r = {}

    def op(self, eng, fn, reads=(), writes=()):
        waits = self._waits(eng, eng, reads, writes)
        st = self.stream[eng]
        idx = st["n"]
        st["n"] += 1
        self._commit(eng, idx, reads, writes)
        self.ops[eng].append((waits, fn, st["sem"], 1))

    def dma(self, eng, out, in_, reads=(), writes=(), **kw):
        s = "%sq%d" % (eng, self.dnext[eng] % self.dpool[eng])
        self.dnext[eng] += 1
        waits = self._waits(eng, None, reads, writes)
        st = self.stream[s]
        idx = st["n"]
        if idx > 0 and self.known[eng].get(s, -1) < idx - 1:
            self.known[eng][s] = idx - 1
            waits.append((st["sem"], idx * 16))
        st["n"] += 1
        self._commit(s, idx, reads, writes)
        self.ops[eng].append((waits, lambda e: e.dma_start(out=out, in_=in_, **kw), st["sem"], 16))

    def raw(self, eng, fn):
        self.ops[eng].append(([], fn, None, 0))

    def flush(self):
        for e in self.ENG:
            waits = []
            for s, st in self.stream.items():
                if st["n"] == 0:
                    continue
                if self.known[e].get(s, -1) >= st["n"] - 1:
                    continue
                self.known[e][s] = st["n"] - 1
                waits.append((st["sem"], st["n"] * st["inc"]))
            self.ops[e].append((waits, None, None, 0))
        ops = self.ops
        self.ops = {e: [] for e in self.ENG}

        def body(lst):
            def f(e):
                for waits, fn, sem, inc in lst:
                    for (ws, wv) in waits:
                        e.wait_ge(ws, wv)
                    if fn is not None:
                        ins = fn(e)
                        if sem is not None:
                            ins.then_inc(sem, inc)
            return f

        with self.nc.Block() as blk:
            blk.sync(body(ops["sp"]))
            blk.tensor(body(ops["pe"]))
            blk.scalar(body(ops["act"]))
            blk.vector(body(ops["dve"]))
            blk.gpsimd(body(ops["pool"]))
        self.nflush = getattr(self, "nflush", 0) + 1
        if getattr(self, "stop", None) is not None and self.nflush >= self.stop:
            raise StopStage()


def new_nc():
    return bass.Bass("TRN2", target_bir_lowering=False)


def din(nc, name, shape, dt=F32):
    return nc.dram_tensor(name, list(shape), dt, kind="ExternalInput").ap()


def dout(nc, name, shape, dt=F32):
    return nc.dram_tensor(name, list(shape), dt, kind="ExternalOutput").ap()


def stage_k0(nc, S, wslice, bias, cvec, mods_out):
    with (
        nc.sbuf_tensor("k0_w0", [128, 16, 768], F32) as w0,
        nc.sbuf_tensor("k0_w1", [128, 16, 768], F32) as w1,
        nc.sbuf_tensor("k0_cv", [128, 16, 3], F32) as cv,
        nc.sbuf_tensor("k0_sc", [128, 16, 3], F32) as sc,
        nc.sbuf_tensor("k0_b", [128, 24], F32) as bt,
        nc.sbuf_tensor("k0_o", [128, 24, 3], F32) as ot,
        nc.psum_tensor("k0_ps", [128, 8, 512], F32) as ps,
    ):
        wt = [w0, w1]
        bw = [bufs(16), bufs(16)]
        b_cv, b_sc, b_b, b_o = bufs(4)
        bps = bufs(8, True)
        S.dma("sp", cv[:], cvec, writes=[b_cv])
        S.dma("sp", bt[:], bias, writes=[b_b])
        S.op("act", lambda e: e.activation(out=sc[:], in_=cv[:], func=AF.Silu), reads=[b_cv], writes=[b_sc])
        for g in range(4):
            t = wt[g % 2]
            for kc in range(16):
                S.dma("sp" if kc % 2 == 0 else "pool", t[:, kc, :], wslice[kc, :, g * 768:(g + 1) * 768],
                      writes=[bw[g % 2][kc]])
            for i in range(6):
                lt = g * 6 + i
                bank = lt % 8
                for kc in range(16):
                    S.op("pe", lambda e, t=t, kc=kc, i=i, bank=bank: e.matmul(
                        ps[:, bank, 0:3], lhsT=t[:, kc, i * 128:(i + 1) * 128], rhs=sc[:, kc, :],
                        start=(kc == 0), stop=(kc == 15)),
                        reads=[bw[g % 2][kc], b_sc], writes=[bps[bank]])
                S.op("dve", lambda e, lt=lt, bank=bank: e.tensor_scalar(
                    out=ot[:, lt, :], in0=ps[:, bank, 0:3], scalar1=bt[:, lt:lt + 1], scalar2=None, op0=ALU.add),
                    reads=[bps[bank], b_b], writes=[b_o])
        S.dma("sp", mods_out, ot[:], reads=[b_o])
        S.flush()


def build_k0():
    nc = new_nc()
    S = Sched(nc)
    w = din(nc, "wslice", [16, 128, 3072])
    b = din(nc, "bias", [128, 24])
    c = din(nc, "cvec", [128, 16, 3])
    o = dout(nc, "mods", [128, 24, 3])
    stage_k0(nc, S, w, b, c, o)
    return nc


BLKS = [(0, 64), (64, 576), (576, 1088)]


def stage_ca(nc, S, do_c, final, x_in, x_out, yx_in, wout, mods, ng, hx_out, g0, ga, nmods=96):
    with (
        nc.sbuf_tensor("ca_x", [128, 16, OWN], F32) as X,
        nc.sbuf_tensor("ca_mods", [128, nmods, 2], F32) as MODS,
        nc.sbuf_tensor("ca_ng", [128, 16], F32) as NG,
        nc.sbuf_tensor("ca_A", [128, 2, 16], F32) as AM,
        nc.sbuf_tensor("ca_eps", [128, 1], F32) as EPT,
        nc.sbuf_tensor("ca_ones", [128, 128], BF16) as ONES,
        nc.psum_tensor("ca_ps", [128, 8, 512], F32) as ps,
    ):
        bX = [[Buf() for _ in range(3)] for _ in range(16)]
        b_mods, b_ng, b_A, b_c = bufs(4)
        bps = bufs(8, True)
        for ch in range(16):
            S.dma("sp" if ch % 2 == 0 else "pool", X[:, ch, :], x_in[:, ch, :], writes=bX[ch])
        S.dma("sp", MODS[:], mods, writes=[b_mods])
        S.dma("sp", NG[:], ng, writes=[b_ng])
        S.op("pool", lambda e: e.memset(EPT[:], EPS), writes=[b_c])
        S.op("pool", lambda e: e.memset(ONES[:], 1.0 / D), writes=[b_c])

        if do_c:
            with (
                nc.sbuf_tensor("ca_yx", [128, 16, OWN], BF16) as YX,
                nc.sbuf_tensor("ca_wf0", [128, 16, 128], F32) as wf0,
                nc.sbuf_tensor("ca_wf1", [128, 16, 128], F32) as wf1,
                nc.sbuf_tensor("ca_wb0", [128, 16, 128], BF16) as wb0,
                nc.sbuf_tensor("ca_wb1", [128, 16, 128], BF16) as wb1,
            ):
                b_yx = bufs(16)
                bwf = bufs(2)
                bwb = bufs(2)
                wf = [wf0, wf1]
                wb = [wb0, wb1]
                for kc in range(16):
                    S.dma("sp" if kc % 2 == 0 else "pool", YX[:, kc, :], yx_in[kc], writes=[b_yx[kc]])
                wv = wout.rearrange("(kc p) n -> p kc n", p=128)
                for ct in range(16):
                    sl = ct % 2
                    S.dma("sp", wf[sl][:], wv[:, :, ct * 128:(ct + 1) * 128], writes=[bwf[sl]])
                    S.op("pool" if ct % 2 == 0 else "act",
                         (lambda e, sl=sl: e.tensor_copy(out=wb[sl][:], in_=wf[sl][:])) if ct % 2 == 0 else
                         (lambda e, sl=sl: e.activation(out=wb[sl][:], in_=wf[sl][:], func=AF.Copy)),
                         reads=[bwf[sl]], writes=[bwb[sl]])
                    for bi, (t0, t1) in enumerate(BLKS):
                        bank = (ct * 3 + bi) % 8
                        n = t1 - t0
                        for kc in range(16):
                            S.op("pe", lambda e, sl=sl, kc=kc, bank=bank, t0=t0, t1=t1, n=n: e.matmul(
                                ps[:, bank, 0:n], lhsT=wb[sl][:, kc, :], rhs=YX[:, kc, t0:t1],
                                start=(kc == 0), stop=(kc == 15)),
                                reads=[bwb[sl], b_yx[kc]], writes=[bps[bank]])
                        v = 1 if bi == 0 else 0
                        S.op("dve", lambda e, ct=ct, bank=bank, t0=t0, t1=t1, n=n, v=v: e.scalar_tensor_tensor(
                            out=X[:, ct, t0:t1], in0=ps[:, bank, 0:n], scalar=MODS[:, g0 + ct, v:v + 1],
                            in1=X[:, ct, t0:t1], op0=ALU.mult, op1=ALU.add),
                            reads=[bps[bank], b_mods, bX[ct][bi]], writes=[bX[ct][bi]])
                if x_out is not None:
                    for ch in range(16):
                        S.dma("sp" if ch % 2 == 0 else "pool", x_out[:, ch, :], X[:, ch, :], reads=bX[ch])
                S.flush()

        with (
            nc.sbuf_tensor("ca_sq", [128, 3, 512], BF16) as SQ,
            nc.sbuf_tensor("ca_rt", [128, 512], F32) as RT,
            nc.sbuf_tensor("ca_rs", [128, 512], F32) as RS,
            nc.sbuf_tensor("ca_tmp", [128, 3, 512], F32) as TMP,
            nc.sbuf_tensor("ca_st", [128, 4, 512], F32 if final else BF16) as ST,
        ):
            bsq = bufs(3)
            btmp = bufs(3)
            bst = bufs(4)
            b_rt, b_rs = bufs(2)
            if final:
                S.op("dve", lambda e: e.tensor_copy(out=AM[:, 0, :], in_=NG[:]), reads=[b_ng], writes=[b_A])
            else:
                for v in range(2):
                    S.op("dve", lambda e, v=v: e.scalar_tensor_tensor(
                        out=AM[:, v, :], in0=MODS[:, ga + 16:ga + 32, v], scalar=1.0, in1=NG[:],
                        op0=ALU.add, op1=ALU.mult), reads=[b_mods, b_ng], writes=[b_A])
            k = 0
            for bi, (t0, t1) in enumerate(BLKS):
                if final and bi == 0:
                    continue
                n = t1 - t0
                v = 1 if bi == 0 else 0
                bank = bi
                for ch in range(16):
                    sl = k % 3
                    k += 1
                    S.op("act", lambda e, sl=sl, ch=ch, t0=t0, t1=t1, n=n: e.activation(
                        out=SQ[:, sl, 0:n], in_=X[:, ch, t0:t1], func=AF.Square),
                        reads=[bX[ch][bi]], writes=[bsq[sl]])
                    S.op("pe", lambda e, sl=sl, ch=ch, bank=bank, n=n: e.matmul(
                        ps[:, bank, 0:n], lhsT=ONES[:], rhs=SQ[:, sl, 0:n], start=(ch == 0), stop=(ch == 15)),
                        reads=[bsq[sl], b_c], writes=[bps[bank]])
                S.op("act", lambda e, bank=bank, n=n: e.activation(
                    out=RT[:, 0:n], in_=ps[:, bank, 0:n], func=AF.Sqrt, bias=EPT[:], scale=1.0),
                    reads=[bps[bank], b_c], writes=[b_rt])
                S.op("dve", lambda e, n=n: e.reciprocal(out=RS[:, 0:n], in_=RT[:, 0:n]), reads=[b_rt], writes=[b_rs])
                for ch in range(16):
                    sl = ch % 3
                    so = ch % 4
                    S.op("dve", lambda e, sl=sl, ch=ch, t0=t0, t1=t1, n=n: e.tensor_tensor(
                        out=TMP[:, sl, 0:n], in0=X[:, ch, t0:t1], in1=RS[:, 0:n], op=ALU.mult),
                        reads=[bX[ch][bi], b_rs], writes=[btmp[sl]])
                    if final:
                        S.op("act", lambda e, sl=sl, so=so, ch=ch, n=n: e.activation(
                            out=ST[:, so, 0:n], in_=TMP[:, sl, 0:n], func=AF.Copy, scale=AM[:, 0, ch:ch + 1]),
                            reads=[btmp[sl], b_A], writes=[bst[so]])
                        S.dma("sp", hx_out[ch * 128:(ch + 1) * 128, t0 - 64:t1 - 64], ST[:, so, 0:n], reads=[bst[so]])
                    else:
                        S.op("act", lambda e, sl=sl, so=so, ch=ch, n=n, v=v: e.activation(
                            out=ST[:, so, 0:n], in_=TMP[:, sl, 0:n], func=AF.Identity,
                            scale=AM[:, v, ch:ch + 1], bias=MODS[:, ga + ch, v:v + 1]),
                            reads=[btmp[sl], b_A, b_mods], writes=[bst[so]])
                        S.dma("sp", hx_out[ch * 128:(ch + 1) * 128, t0:t1], ST[:, so, 0:n], reads=[bst[so]])
            S.flush()


def build_ca(do_c, final):
    nc = new_nc()
    S = Sched(nc)
    x_in = din(nc, "x_in", [128, 16, OWN])
    mods = din(nc, "mods", [128, 96, 2])
    ng = din(nc, "ng", [128, 16])
    yx = wout = x_out = None
    if do_c:
        yx = din(nc, "yx", [16, 128, OWN], BF16)
        wout = din(nc, "wout", [2048, 2048])
        if not final:
            x_out = dout(nc, "x_out", [128, 16, OWN])
    if final:
        hx = dout(nc, "out", [2048, 1024], F32)
    else:
        hx = dout(nc, "hx", [2048, OWN], BF16)
    stage_ca(nc, S, do_c, final, x_in, x_out, yx, wout, mods, ng, hx, 32, 48)
    return nc


def own_tokens(x_b, ctx_b, j):
    lat = x_b.reshape(64, 64, -1)[:, 16 * j:16 * j + 16, :].transpose(1, 0, 2).reshape(1024, -1)
    return np.concatenate([ctx_b[64 * j:64 * j + 64], lat], axis=0)


def to_fm(tok):
    n = tok.shape[0]
    return np.ascontiguousarray(tok.T.reshape(16, 128, n).transpose(1, 0, 2))


def vec_fm(v):
    return np.ascontiguousarray(v.reshape(16, 128).T)


_PROGS = {}


def prog(key, builder):
    if key not in _PROGS:
        _PROGS[key] = builder()
    return _PROGS[key]


def run(nc, in_maps):
    res = run_bass_kernel_spmd(nc, in_maps, core_ids=list(range(NCORES)))
    return res.results


def host_k0(inp):
    w_mod, b_mod = inp["w_mod"], inp["b_mod"]
    cv = np.stack([inp["c"][0], inp["c"][1], inp["c_ctx"]], axis=0)
    cvec = np.ascontiguousarray(cv.reshape(3, 16, 128).transpose(2, 1, 0))
    maps = []
    for i in range(NCORES):
        l, hf = i // 2, i % 2
        ws = np.ascontiguousarray(w_mod[l][:, hf * 3072:(hf + 1) * 3072]).reshape(16, 128, 3072)
        bs = np.ascontiguousarray(b_mod[l][hf * 3072:(hf + 1) * 3072].reshape(24, 128).T)
        maps.append({"wslice": ws, "bias": bs, "cvec": cvec})
    res = run(prog("k0", build_k0), maps)
    return np.concatenate([r["mods"] for r in res], axis=1)


def mods_for(modsAll, b, lc, la):
    out = np.zeros((128, 96, 2), np.float32)
    if lc is not None:
        out[:, 0:48, :] = modsAll[:, lc * 48:(lc + 1) * 48, :][:, :, [b, 2]]
    if la is not None:
        out[:, 48:96, :] = modsAll[:, la * 48:(la + 1) * 48, :][:, :, [b, 2]]
    return out


def host_a0(inp, modsAll):
    maps = []
    for core in range(NCORES):
        b, j = core // 4, core % 4
        xt = to_fm(own_tokens(inp["x"][b], inp["ctx"][b], j))
        maps.append({"x_in": xt, "mods": mods_for(modsAll, b, None, 0), "ng": vec_fm(inp["norm_g"][0])})
    res = run(prog("a0", lambda: build_ca(False, False)), maps)
    return maps, [r["hx"] for r in res]


NSM = 292
LN_KSCALE = float(-0.5 * np.log(128.0))
HXBLK = [(0, 256, [(r, 0, 64, 64 * r) for r in range(4)])] + \
        [(256 + 512 * i, 512, [(i // 2, 64 + 512 * (i % 2), 512, 0)]) for i in range(8)]


def hx_view(hx_all, r):
    return hx_all[r].rearrange("(kc p) t -> p kc t", p=128)


def load_hx_block(S, HXB, slot, bhx, hx_all, bi):
    pos, n, pieces = HXBLK[bi]
    for (r, so, pn, do) in pieces:
        S.dma("sp" if (bi + r) % 2 == 0 else "pool", HXB[:, slot, :, do:do + pn], hx_view(hx_all, r)[:, :, so:so + pn],
              writes=[bhx[slot]])


def load_weights_bf16(S, W, WST, wsrc, ncols, bw, bst):
    for kc in range(16):
        sl = kc % 2
        S.dma("sp", WST[:, sl, 0:ncols], wsrc[kc], writes=[bst[sl]])
        if kc % 2 == 0:
            S.op("pool", lambda e, kc=kc, sl=sl: e.tensor_copy(out=W[:, kc, 0:ncols], in_=WST[:, sl, 0:ncols]),
                 reads=[bst[sl]], writes=[bw])
        else:
            S.op("act", lambda e, kc=kc, sl=sl: e.activation(out=W[:, kc, 0:ncols], in_=WST[:, sl, 0:ncols], func=AF.Copy),
                 reads=[bst[sl]], writes=[bw])


def padpos(bi):
    pos, n, _ = HXBLK[bi]
    return (CT0 + pos) if bi == 0 else (LT0 + pos - 256)


def conv4(S, X, CV, SM, c0, bX, bCV, b_sm):
    n = 4356
    S.op("act", lambda e: e.activation(out=CV[:, 1:1 + n], in_=X[:, 0:n], func=AF.Identity,
                                       scale=SM[:, c0:c0 + 1], bias=SM[:, c0 + 4:c0 + 5]),
         reads=[bX, b_sm], writes=[bCV])
    for j in range(1, 4):
        S.op("dve", lambda e, j=j: e.scalar_tensor_tensor(
            out=CV[:, 1:1 + n], in0=X[:, j:j + n], scalar=SM[:, c0 + j:c0 + j + 1], in1=CV[:, 1:1 + n],
            op0=ALU.mult, op1=ALU.add), reads=[bX, b_sm, bCV], writes=[bCV])


def stage_b(nc, S, hx_all, wM, wR, sm, cst, wrg, yx_out):
    with (
        nc.sbuf_tensor("b_sm", [128, NSM], F32) as SM,
        nc.sbuf_tensor("b_cst", [128, 6, 128], F32) as CST,
        nc.sbuf_tensor("b_mskf", [128, 128], BF16) as MSKF,
        nc.sbuf_tensor("b_mskb", [128, 128], BF16) as MSKB,
        nc.sbuf_tensor("b_idb", [128, 128], BF16) as IDB,
        nc.psum_tensor("b_ps", [128, 7, 512], F32) as ps,
    ):
        b_sm, b_cst, b_msk = bufs(3)
        bps = bufs(7, True)
        S.dma("sp", SM[:], sm, writes=[b_sm])
        S.dma("sp", CST[:], cst, writes=[b_cst])
        S.op("dve", lambda e: e.tensor_copy(out=MSKF[:], in_=CST[:, 2, :]), reads=[b_cst], writes=[b_msk])
        S.op("dve", lambda e: e.tensor_copy(out=MSKB[:], in_=CST[:, 3, :]), reads=[b_cst], writes=[b_msk])
        S.op("dve", lambda e: e.tensor_copy(out=IDB[:], in_=CST[:, 4, :]), reads=[b_cst], writes=[b_msk])
        S.idb = IDB
        stage_b_mlstm(nc, S, hx_all, wM, yx_out, SM, CST, MSKF, MSKB, ps, b_sm, b_cst, b_msk, bps)
        stage_b_rglru(nc, S, hx_all, wR, wrg, yx_out, SM, CST, ps, b_sm, b_cst, bps)


def stage_b_mlstm(nc, S, hx_all, wM, yx_out, SM, CST, MSKF, MSKB, ps, b_sm, b_cst, b_msk, bps):
    with (
        nc.sbuf_tensor("m_qt", [128, T], BF16) as QT,
        nc.sbuf_tensor("m_qt0", [128, T], BF16) as QT0,
        nc.sbuf_tensor("m_qt1", [128, T], BF16) as QT1,
        nc.sbuf_tensor("m_kt", [128, T], BF16) as KT,
        nc.sbuf_tensor("m_va", [128, NT, 257], BF16) as VA,
        nc.sbuf_tensor("m_v0", [128, NT, 257], BF16) as V0,
        nc.sbuf_tensor("m_v1", [128, NT, 257], BF16) as V1,
        nc.sbuf_tensor("m_og", [128, NT, 256], BF16) as OG,
        nc.sbuf_tensor("m_gt", [128, 4, NT], F32) as GT,
    ):
        b_qt, b_kt, b_q01 = bufs(3)
        b_va = bufs(NT)
        b_og = bufs(NT)
        b_gt = Buf()
        b_vinit = Buf()
        with (
            nc.sbuf_tensor("m_w", [128, 16, 256], BF16) as W,
            nc.sbuf_tensor("m_wst", [128, 2, 256], F32) as WST,
            nc.sbuf_tensor("m_hxb", [128, 2, 16, 512], BF16) as HXB,
            nc.sbuf_tensor("m_qp", [128, PADT], F32) as QP,
            nc.sbuf_tensor("m_kp", [128, PADT], F32) as KP,
            nc.sbuf_tensor("m_cv", [128, PADT], F32) as CV,
        ):
            bw, b_qp, b_kp, b_cv = bufs(4)
            bst = bufs(2)
            bhx = bufs(2)
            load_weights_bf16(S, W, WST, wM[:, :, 0:256], 256, bw, bst)
            S.op("pool", lambda e: e.memset(QP[:], 0.0), writes=[b_qp])
            S.op("pool", lambda e: e.memset(KP[:], 0.0), writes=[b_kp])
            S.op("pool", lambda e: e.memset(QT0[:], 0.0), writes=[b_q01])
            S.op("pool", lambda e: e.memset(QT1[:], 0.0), writes=[b_q01])
            for bi in range(9):
                slot = bi % 2
                load_hx_block(S, HXB, slot, bhx, hx_all, bi)
                pos, n, _ = HXBLK[bi]
                pp = padpos(bi)
                for fm, (dst, bdst) in enumerate([(QP, b_qp), (KP, b_kp)]):
                    bank = (bi * 2 + fm) % 6
                    for kc in range(16):
                        S.op("pe", lambda e, fm=fm, kc=kc, bank=bank, slot=slot, n=n: e.matmul(
                            ps[:, bank, 0:n], lhsT=W[:, kc, fm * 128:(fm + 1) * 128], rhs=HXB[:, slot, kc, 0:n],
                            start=(kc == 0), stop=(kc == 15)), reads=[bw, bhx[slot]], writes=[bps[bank]])
                    S.op("act", lambda e, dst=dst, bank=bank, pp=pp, n=n: e.activation(
                        out=dst[:, pp:pp + n], in_=ps[:, bank, 0:n], func=AF.Copy),
                        reads=[bps[bank]], writes=[bdst])
            for (X, bX, c0, dstT, bdT) in [(QP, b_qp, 0, QT, b_qt), (KP, b_kp, 5, KT, b_kt)]:
                conv4(S, X, CV, SM, c0, bX, b_cv, b_sm)
                S.op("act", lambda e, dstT=dstT: e.activation(out=dstT[:, 0:256], in_=CV[:, CT0:CT0 + 256], func=AF.Silu),
                     reads=[b_cv], writes=[bdT])
                S.op("act", lambda e, dstT=dstT: e.activation(out=dstT[:, 256:T], in_=CV[:, LT0:LT0 + 4096], func=AF.Silu),
                     reads=[b_cv], writes=[bdT])
            qv = QT[:].rearrange("p (t c j) -> p t c j", c=2, j=64)
            q0v = QT0[:].rearrange("p (t c j) -> p t c j", c=2, j=64)
            q1v = QT1[:].rearrange("p (t c j) -> p t c j", c=2, j=64)
            S.op("pool", lambda e: e.tensor_copy(out=q0v[:, :, 0, :], in_=qv[:, :, 0, :]), reads=[b_qt], writes=[b_q01])
            S.op("pool", lambda e: e.tensor_copy(out=q1v[:, :, 1, :], in_=qv[:, :, 1, :]), reads=[b_qt], writes=[b_q01])
            S.flush()
        with (
            nc.sbuf_tensor("m_w2", [128, 16, 772], BF16) as W,
            nc.sbuf_tensor("m_wst2", [128, 2, 772], F32) as WST,
            nc.sbuf_tensor("m_hxb2", [128, 2, 16, 512], BF16) as HXB,
            nc.sbuf_tensor("m_so", [128, 2, 256], F32) as SO,
            nc.sbuf_tensor("m_sz", [128, 2, 256], F32) as SZ,
            nc.sbuf_tensor("m_sz2", [128, 2, 256], F32) as SZ2,
        ):
            bw = Buf()
            bst = bufs(2)
            bhx = bufs(2)
            bso = bufs(2)
            bsz = bufs(2)
            bsz2 = bufs(2)
            load_weights_bf16(S, W, WST, wM[:, :, 256:1028], 772, bw, bst)
            S.op("pool", lambda e: e.memset(V0[:], 0.0), writes=[b_vinit])
            S.op("pool", lambda e: e.memset(V1[:], 0.0), writes=[b_vinit])
            S.op("pool", lambda e: e.memset(VA[:, :, 256:257], 1.0), writes=[b_vinit])
            if "nomemset" not in DBG:
                S.op("pool", lambda e: e.memset(V0[0:64, :, 256:257], 1.0), writes=[b_vinit])
                S.op("pool", lambda e: e.memset(V1[64:128, :, 256:257], 1.0), writes=[b_vinit])
            k = 0
            for bi in range(9):
                slot = bi % 2
                load_hx_block(S, HXB, slot, bhx, hx_all, bi)
                pos, n, _ = HXBLK[bi]
                for tt in range(n // 128):
                    tile = pos // 128 + tt
                    s2 = k % 2
                    b1 = (2 * k) % 6
                    b2 = (2 * k + 1) % 6
                    k += 1
                    for kc in range(16):
                        S.op("pe", lambda e, kc=kc, b1=b1, slot=slot, tt=tt: e.matmul(
                            ps[:, b1, 0:512], lhsT=HXB[:, slot, kc, tt * 128:(tt + 1) * 128], rhs=W[:, kc, 0:512],
                            start=(kc == 0), stop=(kc == 15)), reads=[bw, bhx[slot]], writes=[bps[b1]])
                    for kc in range(16):
                        S.op("pe", lambda e, kc=kc, b2=b2, slot=slot, tt=tt: e.matmul(
                            ps[:, b2, 0:260], lhsT=HXB[:, slot, kc, tt * 128:(tt + 1) * 128], rhs=W[:, kc, 512:772],
                            start=(kc == 0), stop=(kc == 15)), reads=[bw, bhx[slot]], writes=[bps[b2]])
                    S.op("dve", lambda e, tile=tile, b1=b1: e.tensor_copy(out=VA[:, tile, 0:256], in_=ps[:, b1, 0:256]),
                         reads=[bps[b1], b_vinit], writes=[b_va[tile]])
                    if "nov01" not in DBG:
                        S.op("dve", lambda e, tile=tile, b1=b1: e.tensor_copy(out=V0[0:64, tile, 0:256], in_=ps[0:64, b1, 0:256]),
                             reads=[bps[b1], b_vinit], writes=[b_va[tile]])
                        S.op("dve", lambda e, tile=tile, b1=b1: e.tensor_copy(out=V1[64:128, tile, 0:256], in_=ps[64:128, b1, 0:256]),
                             reads=[bps[b1], b_vinit], writes=[b_va[tile]])
                    if "noact" in DBG:
                        continue
                    S.op("act", lambda e, s2=s2, b1=b1: e.activation(out=SO[:, s2, :], in_=ps[:, b1, 256:512], func=(AF.Copy if "copyf" in DBG else AF.Sigmoid)),
                         reads=[bps[b1]] + ([b_va[tile]] if "serial" in DBG else []), writes=[bso[s2]])
                    S.op("act", lambda e, s2=s2, b2=b2: e.activation(out=SZ[:, s2, :], in_=ps[:, b2, 0:256], func=(AF.Copy if "copyf" in DBG or "copyz" in DBG else AF.Silu)),
                         reads=[bps[b2]], writes=[bsz[s2]])
                    if "nogt" not in DBG:
                        S.op("act", lambda e, tile=tile, b2=b2: e.activation(out=GT[:, :, tile], in_=ps[:, b2, 256:260], func=AF.Copy),
                             reads=[bps[b2]], writes=[b_gt])
                    if "nopool" in DBG:
                        continue
                    S.op("pool", lambda e, s2=s2: e.tensor_tensor(out=SZ2[:, s2, :], in0=SZ[:, s2, :], in1=SM[:, 36:292], op=ALU.mult),
                         reads=[bsz[s2], b_sm], writes=[bsz2[s2]])
                    S.op("dve", lambda e, s2=s2, tile=tile: e.tensor_tensor(out=OG[:, tile, :], in0=SO[:, s2, :], in1=SZ2[:, s2, :], op=ALU.mult),
                         reads=[bso[s2], bsz2[s2]], writes=[b_og[tile]])
            S.flush()
        mlstm_mixer(nc, S, yx_out, SM, CST, MSKF, MSKB, ps, b_sm, b_cst, b_msk, bps,
                    QT, QT0, QT1, KT, VA, V0, V1, OG, GT, b_qt, b_kt, b_q01, b_va, b_og, b_gt)


def rev(ap):
    return ap[:, ::-1]


SP_, IG_, TOT_, INC_, BN_, A_, CMB0_, CMB1_, PM_, PINC_, PEXC_, MP0_, MP1_, MM0_, MM1_, DEC0_, DEC1_, AL_, EE_, D_ = range(20)
ORDER_F = list(range(NT))
ORDER_B = [1, 0] + list(range(NT - 1, 1, -1))


def mlstm_mixer(nc, S, yx_out, SM, CST, MSKF, MSKB, ps, b_sm, b_cst, b_msk, bps,
                QT, QT0, QT1, KT, VA, V0, V1, OG, GT, b_qt, b_kt, b_q01, b_va, b_og, b_gt):
    with ExitStack() as es:
        KF = es.enter_context(nc.sbuf_tensor("x_kf", [128, NT, 128], BF16))
        KB = es.enter_context(nc.sbuf_tensor("x_kb", [128, NT, 128], BF16))
        HACC = es.enter_context(nc.sbuf_tensor("x_hacc", [128, NT, 256], F32))
        G = es.enter_context(nc.sbuf_tensor("x_g", [128, 2, 20, NT], F32))
        CM = es.enter_context(nc.sbuf_tensor("x_cm", [128, 2, 257], F32))
        CB = es.enter_context(nc.sbuf_tensor("x_cb", [128, 2, 2, 257], BF16))
        SSB = es.enter_context(nc.sbuf_tensor("x_ssb", [128, 4, 128], BF16))
        DN = es.enter_context(nc.sbuf_tensor("x_dn", [128, 4, 4], F32))
        YM = es.enter_context(nc.sbuf_tensor("x_ym", [128, 2, 256], BF16))
        YMT = es.enter_context(nc.sbuf_tensor("x_ymt", [128, 2, T], BF16))
        SQJ = es.enter_context(nc.sbuf_tensor("x_sqj", [128, 256], F32))
        SS = es.enter_context(nc.sbuf_tensor("x_ss", [128, NT], F32))
        RTT = es.enter_context(nc.sbuf_tensor("x_rt", [128, NT], F32))
        RSTD = es.enter_context(nc.sbuf_tensor("x_rs", [128, NT], F32))
        DG = es.enter_context(nc.sbuf_tensor("x_dg", [34, 2, 34], F32))
        CMT = es.enter_context(nc.sbuf_tensor("x_cmt", [34, 2], F32))
        NB = es.enter_context(nc.sbuf_tensor("x_nb", [128, 4], F32))
        ONER = es.enter_context(nc.sbuf_tensor("x_oner", [128, NT], F32))
        EPT = es.enter_context(nc.sbuf_tensor("x_eps", [128, 1], F32))
        psb = es.enter_context(nc.psum_tensor("x_psb", [128, 1024], BF16))
        b_g = [Buf(), Buf()]
        b_gq = [Buf(), Buf()]
        b_misc, b_dg, b_cmt, b_nb = bufs(4)
        b_psb = bufs(8, True)
        b_kf = bufs(NT)
        b_kb = bufs(NT)
        b_hacc = bufs(NT)
        b_cm = bufs(2)
        b_cb = [bufs(2), bufs(2)]
        b_ssb = bufs(4)
        b_dn = bufs(4)
        b_ym = bufs(2)
        b_ymt = bufs(2)
        b_ss, b_sqj, b_rs = bufs(3)
        GB = 6
        S.op("pool", lambda e: e.memset(ONER[:], 1.0), writes=[b_misc])
        S.op("pool", lambda e: e.memset(EPT[:], EPS), writes=[b_misc])
        S.op("pool", lambda e: e.memset(CM[:], 0.0), writes=b_cm)
        S.op("dve", lambda e: e.tensor_scalar(out=NB[:], in0=SM[:, 20:24], scalar1=-1.0, scalar2=None, op0=ALU.mult),
             reads=[b_sm], writes=[b_nb])
        for d in range(2):
            fh, sh = d, 1 - d
            tri = CST[:, d, :]

            def g(s, d=d):
                return G[:, d, s, :]
            R = [b_g[d]]
            S.op("act", lambda e, d=d, g=g: e.activation(out=g(SP_), in_=GT[:, 2 * d + 1, :], func=AF.Exp, scale=-1.0,
                                                         bias=NB[:, 2 * d + 1:2 * d + 2]), reads=[b_gt, b_nb], writes=R)
            S.op("act", lambda e, g=g: e.activation(out=g(SP_), in_=g(SP_), func=AF.Ln, bias=1.0, scale=1.0), reads=R, writes=R)
            S.op("dve", lambda e, d=d, g=g: e.tensor_scalar(out=g(IG_), in0=GT[:, 2 * d, :], scalar1=SM[:, 20 + 2 * d:21 + 2 * d],
                                                            scalar2=None, op0=ALU.add), reads=[b_gt, b_sm], writes=R)
            S.op("pe", lambda e, g=g, tri=tri: e.matmul(ps[:, GB, 0:34], lhsT=tri, rhs=g(SP_), start=True, stop=True),
                 reads=R + [b_cst], writes=[bps[GB]])
            S.op("pe", lambda e, g=g: e.matmul(ps[:, GB, 34:68], lhsT=CST[:, 5, :], rhs=g(SP_), start=True, stop=True),
                 reads=R + [b_cst], writes=[bps[GB]])
            S.op("dve", lambda e, g=g: e.tensor_copy(out=g(TOT_), in_=ps[:, GB, 34:68]), reads=[bps[GB]], writes=R)

            def scan(out_s, in_s, op0, op1, d=d, g=g, data0_ones=False):
                def emit(lo, hi, init, reverse):
                    def f(e):
                        o = g(out_s)[:, lo:hi]
                        i1 = g(in_s)[:, lo:hi]
                        i0 = ONER[:, lo:hi] if data0_ones else g(in_s)[:, lo:hi]
                        if reverse:
                            o, i0, i1 = rev(o), rev(i0), rev(i1)
                        return e.tensor_tensor_scan(out=o, data0=i0, data1=i1, initial=init, op0=op0, op1=op1)
                    S.op("dve", f, reads=R + [b_misc], writes=R)
                if d == 0:
                    emit(0, NT, 0.0, False)
                else:
                    emit(0, 2, 0.0, True)
                    emit(2, NT, g(out_s)[:, 0:1], True)
            scan(INC_, TOT_, ALU.mult, ALU.add, data0_ones=True)
            S.op("dve", lambda e, g=g: e.tensor_tensor(out=g(D_), in0=g(INC_), in1=g(TOT_), op=ALU.subtract), reads=R, writes=R)
            S.op("dve", lambda e, g=g: e.tensor_tensor(out=g(BN_), in0=ps[:, GB, 0:34], in1=g(D_), op=ALU.add),
                 reads=R + [bps[GB]], writes=R)
            S.op("dve", lambda e, g=g: e.tensor_tensor(out=g(A_), in0=g(IG_), in1=g(BN_), op=ALU.add), reads=R, writes=R)
            S.op("pe", lambda e, g=g: e.transpose(out=ps[0:34, GB, 128:256], in_=g(A_), identity=CST[:, 4, :]), reads=R, writes=[bps[GB]])
            S.op("dve", lambda e: e.tensor_reduce(out=CMT[:, :], in_=ps[0:34, GB, 128:256].rearrange("p (c j) -> p c j", j=64),
                                                  axis=AX.X, op=ALU.max), reads=[bps[GB]], writes=[b_cmt])
            for cc in range(2):
                S.op("dve", lambda e, cc=cc: e.tensor_scalar(out=DG[:, cc, :], in0=CST[0:34, 4, 0:34], scalar1=CMT[:, cc:cc + 1],
                                                             scalar2=None, op0=ALU.mult), reads=[b_cmt, b_cst], writes=[b_dg])
            S.op("pe", lambda e: e.matmul(ps[:, GB, 256:324], lhsT=CST[0:34, 5, :], rhs=DG[:].rearrange("p c t -> p (c t)"),
                                          start=True, stop=True), reads=[b_dg, b_cst], writes=[bps[GB]])
            S.op("dve", lambda e, d=d: e.tensor_copy(out=G[:, d, CMB0_:CMB1_ + 1, :],
                                                     in_=ps[:, GB, 256:324].rearrange("p (c t) -> p c t", c=2)),
                 reads=[bps[GB]], writes=R)
            S.op("dve", lambda e, g=g: e.tensor_tensor(out=g(PM_), in0=g(CMB0_), in1=g(CMB1_), op=ALU.max), reads=R, writes=R)
            scan(PINC_, PM_, ALU.max, ALU.max)
            if d == 0:
                S.op("dve", lambda e, g=g: e.memset(g(PEXC_)[:, 0:1], 0.0), reads=R, writes=R)
                S.op("dve", lambda e, g=g: e.tensor_copy(out=g(PEXC_)[:, 1:NT], in_=g(PINC_)[:, 0:NT - 1]), reads=R, writes=R)
            else:
                S.op("dve", lambda e, g=g: e.tensor_copy(out=g(PEXC_)[:, 0:1], in_=g(PINC_)[:, 1:2]), reads=R, writes=R)
                S.op("dve", lambda e, g=g: e.memset(g(PEXC_)[:, 1:2], 0.0), reads=R, writes=R)
                S.op("dve", lambda e, g=g: e.tensor_copy(out=g(PEXC_)[:, 2:NT - 1], in_=g(PINC_)[:, 3:NT]), reads=R, writes=R)
                S.op("dve", lambda e, g=g: e.tensor_copy(out=g(PEXC_)[:, NT - 1:NT], in_=g(PINC_)[:, 0:1]), reads=R, writes=R)
            S.op("dve", lambda e, g=g, fh=fh: e.tensor_tensor(out=g(MP0_ + fh), in0=g(PEXC_), in1=g(CMB0_ + fh), op=ALU.max), reads=R, writes=R)
            S.op("dve", lambda e, g=g, sh=sh: e.tensor_copy(out=g(MP0_ + sh), in_=g(PINC_)), reads=R, writes=R)
            S.op("dve", lambda e, g=g, fh=fh: e.tensor_copy(out=g(MM0_ + fh), in_=g(PEXC_)), reads=R, writes=R)
            S.op("dve", lambda e, g=g, fh=fh, sh=sh: e.tensor_copy(out=g(MM0_ + sh), in_=g(MP0_ + fh)), reads=R, writes=R)
            for cc in range(2):
                S.op("dve", lambda e, g=g, cc=cc: e.tensor_tensor(out=g(DEC0_ + cc), in0=g(MM0_ + cc), in1=g(MP0_ + cc), op=ALU.subtract),
                     reads=R, writes=R)
                S.op("act", lambda e, g=g, cc=cc: e.activation(out=g(DEC0_ + cc), in_=g(DEC0_ + cc), func=AF.Exp), reads=R, writes=R + [b_gq[d]])
                rows = slice(64 * cc, 64 * cc + 64)
                S.op("dve", lambda e, d=d, cc=cc, rows=rows: e.tensor_tensor(
                    out=G[rows, d, AL_, :], in0=G[rows, d, A_, :], in1=G[rows, d, MP0_ + cc, :], op=ALU.subtract), reads=R, writes=R)
                S.op("dve", lambda e, d=d, cc=cc, rows=rows: e.tensor_tensor(
                    out=G[rows, d, EE_, :], in0=G[rows, d, BN_, :], in1=G[rows, d, MP0_ + cc, :], op=ALU.subtract), reads=R, writes=R)
            S.op("act", lambda e, g=g: e.activation(out=g(AL_), in_=g(AL_), func=AF.Exp, bias=LN_KSCALE, scale=1.0), reads=R, writes=R + [b_gq[d]])
            S.op("act", lambda e, g=g: e.activation(out=g(EE_), in_=g(EE_), func=AF.Exp), reads=R, writes=R + [b_gq[d]])

        TPV = [ps[:, bk, 0:64].bitcast(BF16) for bk in range(6)] + [psb[:, 0:128]]
        TPB = [bps[bk] for bk in range(6)] + [b_psb[0]]
        b_psb = TPB
        for tile in range(NT):
            sl = tile % 7
            pv = TPV[sl]
            S.op("pe", lambda e, tile=tile, pv=pv: e.transpose(out=pv, in_=KT[:, tile * 128:(tile + 1) * 128], identity=S.idb[:]),
                 reads=[b_kt, b_msk], writes=[b_psb[sl]])
            S.op("act", lambda e, tile=tile, pv=pv: e.activation(out=KF[:, tile, :], in_=pv, func=AF.Copy,
                                                                 scale=G[:, 0, AL_, tile:tile + 1]),
                 reads=[b_psb[sl], b_gq[0]], writes=[b_kf[tile]])
            S.op("dve", lambda e, tile=tile, pv=pv: e.tensor_scalar(out=KB[:, tile, :], in0=pv, scalar1=G[:, 1, AL_, tile:tile + 1],
                                                                   scalar2=None, op0=ALU.mult),
                 reads=[b_psb[sl], b_gq[1]], writes=[b_kb[tile]])

        QH = [QT0, QT1]
        VH = [V0, V1]
        KD = [KF, KB]
        bKD = [b_kf, b_kb]
        MSK = [MSKF, MSKB]
        seen = set()
        step = 0
        for i in range(NT):
            for d in range(2):
                tile = (ORDER_F if d == 0 else ORDER_B)[i]
                fh, sh = d, 1 - d
                cols = slice(tile * 128, (tile + 1) * 128)
                sS = step % 2
                bS, bO, bC = sS, 2 + sS, 4 + sS
                sb = step % 4
                step += 1
                S.op("pe", lambda e, cols=cols, bS=bS: e.matmul(ps[:, bS, 0:128], lhsT=KT[:, cols], rhs=QT[:, cols], start=True, stop=True),
                     reads=[b_kt, b_qt], writes=[bps[bS]])
                S.op("dve", lambda e, d=d, tile=tile, bS=bS, sb=sb: e.scalar_tensor_tensor(
                    out=SSB[:, sb, :], in0=ps[:, bS, 0:128], scalar=G[:, d, AL_, tile:tile + 1], in1=MSK[d][:],
                    op0=ALU.mult, op1=ALU.mult), reads=[bps[bS], b_gq[d], b_msk], writes=[b_ssb[sb]])
                S.op("dve", lambda e, d=d, fh=fh, tile=tile: e.tensor_scalar(
                    out=CB[:, d, 0, :], in0=CM[:, d, :], scalar1=G[:, d, DEC0_ + fh, tile:tile + 1], scalar2=None, op0=ALU.mult),
                    reads=[b_cm[d], b_gq[d]], writes=[b_cb[d][0]])
                S.op("pe", lambda e, tile=tile, bO=bO, sb=sb: e.matmul(ps[:, bO, 0:257], lhsT=SSB[:, sb, :], rhs=VA[:, tile, :], start=True, stop=False),
                     reads=[b_ssb[sb], b_va[tile]], writes=[bps[bO]])
                S.op("pe", lambda e, d=d, fh=fh, cols=cols, bO=bO: e.matmul(ps[:, bO, 0:257], lhsT=QH[fh][:, cols], rhs=CB[:, d, 0, :], start=False, stop=False),
                     reads=[b_q01, b_cb[d][0]], writes=[bps[bO]])
                S.op("pe", lambda e, d=d, fh=fh, tile=tile, bC=bC: e.matmul(ps[:, bC, 0:257], lhsT=KD[d][:, tile, :], rhs=VH[fh][:, tile, :], start=True, stop=True),
                     reads=[bKD[d][tile], b_va[tile]], writes=[bps[bC]])
                S.op("dve", lambda e, d=d, fh=fh, tile=tile, bC=bC: e.scalar_tensor_tensor(
                    out=CM[:, d, :], in0=CM[:, d, :], scalar=G[:, d, DEC0_ + fh, tile:tile + 1], in1=ps[:, bC, 0:257],
                    op0=ALU.mult, op1=ALU.add), reads=[b_cm[d], b_gq[d], bps[bC]], writes=[b_cm[d]])
                S.op("dve", lambda e, d=d, sh=sh, tile=tile: e.tensor_scalar(
                    out=CB[:, d, 1, :], in0=CM[:, d, :], scalar1=G[:, d, DEC0_ + sh, tile:tile + 1], scalar2=None, op0=ALU.mult),
                    reads=[b_cm[d], b_gq[d]], writes=[b_cb[d][1]])
                S.op("pe", lambda e, d=d, sh=sh, cols=cols, bO=bO: e.matmul(ps[:, bO, 0:257], lhsT=QH[sh][:, cols], rhs=CB[:, d, 1, :], start=False, stop=True),
                     reads=[b_q01, b_cb[d][1]], writes=[bps[bO]])
                S.op("pe", lambda e, d=d, sh=sh, tile=tile, bC=bC: e.matmul(ps[:, bC, 0:257], lhsT=KD[d][:, tile, :], rhs=VH[sh][:, tile, :], start=True, stop=True),
                     reads=[bKD[d][tile], b_va[tile]], writes=[bps[bC]])
                S.op("dve", lambda e, d=d, sh=sh, tile=tile, bC=bC: e.scalar_tensor_tensor(
                    out=CM[:, d, :], in0=CM[:, d, :], scalar=G[:, d, DEC0_ + sh, tile:tile + 1], in1=ps[:, bC, 0:257],
                    op0=ALU.mult, op1=ALU.add), reads=[b_cm[d], b_gq[d], bps[bC]], writes=[b_cm[d]])
                S.op("dve", lambda e, bO=bO, sb=sb: e.tensor_scalar(out=DN[:, sb, 0:1], in0=ps[:, bO, 256:257], scalar1=-1.0, scalar2=None, op0=ALU.mult),
                     reads=[bps[bO]], writes=[b_dn[sb]])
                S.op("dve", lambda e, d=d, tile=tile, bO=bO, sb=sb: e.scalar_tensor_tensor(
                    out=DN[:, sb, 1:2], in0=ps[:, bO, 256:257], scalar=G[:, d, EE_, tile:tile + 1], in1=DN[:, sb, 0:1],
                    op0=ALU.max, op1=ALU.max), reads=[bps[bO], b_gq[d], b_dn[sb]], writes=[b_dn[sb]])
                S.op("dve", lambda e, sb=sb: e.reciprocal(out=DN[:, sb, 2:3], in_=DN[:, sb, 1:2]), reads=[b_dn[sb]], writes=[b_dn[sb]])
                if tile not in seen:
                    seen.add(tile)
                    S.op("act", lambda e, tile=tile, bO=bO, sb=sb: e.activation(out=HACC[:, tile, :], in_=ps[:, bO, 0:256], func=AF.Copy,
                                                                            scale=DN[:, sb, 2:3]),
                         reads=[bps[bO], b_dn[sb]], writes=[b_hacc[tile]])
                else:
                    S.op("dve", lambda e, tile=tile, bO=bO, sb=sb: e.scalar_tensor_tensor(
                        out=HACC[:, tile, :], in0=ps[:, bO, 0:256], scalar=DN[:, sb, 2:3], in1=HACC[:, tile, :],
                        op0=ALU.mult, op1=ALU.add), reads=[bps[bO], b_dn[sb], b_hacc[tile]], writes=[b_hacc[tile]])

        for tile in range(NT):
            S.op("act", lambda e, tile=tile: e.activation(out=SQJ[:], in_=HACC[:, tile, :], func=AF.Square, accum_out=SS[:, tile:tile + 1]),
                 reads=[b_hacc[tile]], writes=[b_sqj, b_ss])
        S.op("act", lambda e: e.activation(out=RTT[:], in_=SS[:], func=AF.Sqrt, scale=1.0 / 256.0, bias=EPT[:]), reads=[b_ss, b_misc], writes=[b_rs])
        S.op("dve", lambda e: e.reciprocal(out=RSTD[:], in_=RTT[:]), reads=[b_rs], writes=[b_rs])
        for tile in range(NT):
            sl = tile % 2
            S.op("dve", lambda e, tile=tile, sl=sl: e.scalar_tensor_tensor(
                out=YM[:, sl, :], in0=HACC[:, tile, :], scalar=RSTD[:, tile:tile + 1], in1=OG[:, tile, :], op0=ALU.mult, op1=ALU.mult),
                reads=[b_hacc[tile], b_rs, b_og[tile]], writes=[b_ym[sl]])
            for vt in range(2):
                ss = (tile * 2 + vt) % 7
                pv = TPV[ss]
                S.op("pe", lambda e, sl=sl, vt=vt, pv=pv: e.transpose(out=pv, in_=YM[:, sl, vt * 128:(vt + 1) * 128], identity=S.idb[:]),
                     reads=[b_ym[sl], b_msk], writes=[b_psb[ss]])
                S.op("act", lambda e, tile=tile, vt=vt, pv=pv: e.activation(out=YMT[:, vt, tile * 128:(tile + 1) * 128], in_=pv, func=AF.Copy),
                     reads=[b_psb[ss]], writes=[b_ymt[vt]])
        for vt in range(2):
            S.dma("sp", yx_out[vt * 128:(vt + 1) * 128, :], YMT[:, vt, :], reads=[b_ymt[vt]])
        S.flush()


def stage_b_rglru(nc, S, hx_all, wR, wrg, yx_out, SM, CST, ps, b_sm, b_cst, bps):
    with (
        nc.sbuf_tensor("r_xp", [128, 2, PADT], F32) as XP,
        nc.sbuf_tensor("r_zg", [128, 2, T], BF16) as ZG,
        nc.sbuf_tensor("r_wrg", [128, 8, 128], BF16) as WRG,
        nc.sbuf_tensor("r_kap", [128, 4], F32) as KAP,
        nc.sbuf_tensor("r_one", [128, 1], F32) as ONE1,
    ):
        b_xp = bufs(2)
        b_zg = bufs(2)
        b_wrg, b_kap = bufs(2)
        with (
            nc.sbuf_tensor("r_w", [128, 16, 512], BF16) as W,
            nc.sbuf_tensor("r_wst", [128, 2, 512], F32) as WST,
            nc.sbuf_tensor("r_hxb", [128, 2, 16, 512], BF16) as HXB,
            nc.sbuf_tensor("r_wrgf", [128, 8, 128], F32) as WRGF,
        ):
            bw = Buf()
            bst = bufs(2)
            bhx = bufs(2)
            b_wf = Buf()
            load_weights_bf16(S, W, WST, wR, 512, bw, bst)
            S.dma("pool", WRGF[:], wrg, writes=[b_wf])
            S.op("dve", lambda e: e.tensor_copy(out=WRG[:], in_=WRGF[:]), reads=[b_wf], writes=[b_wrg])
            S.op("pool", lambda e: e.memset(ONE1[:], 1.0), writes=[b_kap])
            S.op("act", lambda e: e.activation(out=KAP[:], in_=SM[:, 32:36], func=AF.Exp, scale=-1.0), reads=[b_sm], writes=[b_kap])
            S.op("act", lambda e: e.activation(out=KAP[:], in_=KAP[:], func=AF.Ln, bias=1.0, scale=1.0), reads=[b_kap], writes=[b_kap])
            S.op("dve", lambda e: e.tensor_scalar(out=KAP[:], in0=KAP[:], scalar1=-8.0, scalar2=None, op0=ALU.mult), reads=[b_kap], writes=[b_kap])
            for ft in range(2):
                S.op("pool", lambda e, ft=ft: e.memset(XP[:, ft, :], 0.0), writes=[b_xp[ft]])
            k = 0
            for bi in range(9):
                slot = bi % 2
                load_hx_block(S, HXB, slot, bhx, hx_all, bi)
                pos, n, _ = HXBLK[bi]
                for mt in range(4):
                    bank = k % 6
                    k += 1
                    ft = mt % 2
                    for kc in range(16):
                        S.op("pe", lambda e, mt=mt, kc=kc, bank=bank, slot=slot, n=n: e.matmul(
                            ps[:, bank, 0:n], lhsT=W[:, kc, mt * 128:(mt + 1) * 128], rhs=HXB[:, slot, kc, 0:n],
                            start=(kc == 0), stop=(kc == 15)), reads=[bw, bhx[slot]], writes=[bps[bank]])
                    if bi == 0:
                        src = ps[:, bank, 0:256]
                        dx = XP[:, ft, CT0:CT0 + 256]
                        dz = ZG[:, ft, 0:256]
                    else:
                        w0 = 8 * (bi - 1)
                        src = ps[:, bank, 0:512].rearrange("p (w r) -> p w r", r=64)
                        dx = XP[:, ft, LT0:LT0 + 4096].rearrange("p (r w) -> p w r", w=64)[:, w0:w0 + 8, :]
                        dz = ZG[:, ft, 256:T].rearrange("p (r w) -> p w r", w=64)[:, w0:w0 + 8, :]
                    if mt < 2:
                        S.op("dve", lambda e, src=src, dx=dx: e.tensor_copy(out=dx, in_=src), reads=[bps[bank]], writes=[b_xp[ft]])
                    else:
                        S.op("act", lambda e, src=src, dz=dz: e.activation(out=dz, in_=src, func=AF.Silu), reads=[bps[bank]], writes=[b_zg[ft]])
            S.flush()
        with (
            nc.sbuf_tensor("r_xc", [128, PADT], F32) as XC,
            nc.sbuf_tensor("r_xcb", [128, PADT], BF16) as XCB,
            nc.sbuf_tensor("r_a", [128, PADT], F32) as AA,
            nc.sbuf_tensor("r_u", [128, PADT], F32) as UU,
            nc.sbuf_tensor("r_t1", [128, 2, 512], F32) as T1,
            nc.sbuf_tensor("r_t2", [128, 2, 512], F32) as T2,
            nc.sbuf_tensor("r_hs", [128, PADT], F32) as HS,
            nc.sbuf_tensor("r_hb", [128, PADT], F32) as HB,
            nc.sbuf_tensor("r_yr", [128, T], BF16) as YR,
        ):
            b_xc, b_xcb, b_aa, b_uu, b_hs, b_hb, b_yr = bufs(7)
            bt1 = bufs(2)
            bt2 = bufs(2)
            PB = [(p0, min(512, PADT - p0)) for p0 in range(0, PADT, 512)]
            k = 0
            for ft in range(2):
                conv4(S, XP[:, ft, :], XC, SM, 10 + 5 * ft, b_xp[ft], b_xc, b_sm)
                S.op("pool", lambda e: e.tensor_copy(out=XCB[:, 1:4357], in_=XC[:, 1:4357]), reads=[b_xc], writes=[b_xcb])
                for d in range(2):
                    for (p0, n) in PB:
                        lo = max(p0, 1)
                        hi = min(p0 + n, 4357)
                        n2 = hi - lo
                        sl = k % 2
                        br, bi_ = (2 * k) % 6, (2 * k + 1) % 6
                        k += 1
                        S.op("pe", lambda e, d=d, ft=ft, br=br, lo=lo, hi=hi, n2=n2: e.matmul(
                            ps[:, br, 0:n2], lhsT=WRG[:, d * 4 + ft, :], rhs=XCB[:, lo:hi], start=True, stop=True),
                            reads=[b_wrg, b_xcb], writes=[bps[br]])
                        S.op("pe", lambda e, d=d, ft=ft, bi_=bi_, lo=lo, hi=hi, n2=n2: e.matmul(
                            ps[:, bi_, 0:n2], lhsT=WRG[:, d * 4 + 2 + ft, :], rhs=XCB[:, lo:hi], start=True, stop=True),
                            reads=[b_wrg, b_xcb], writes=[bps[bi_]])
                        cr = 24 + d * 4 + ft
                        ci = 24 + d * 4 + 2 + ft
                        S.op("act", lambda e, sl=sl, br=br, n2=n2, cr=cr: e.activation(
                            out=T1[:, sl, 0:n2], in_=ps[:, br, 0:n2], func=AF.Sigmoid, bias=SM[:, cr:cr + 1], scale=1.0),
                            reads=[bps[br], b_sm], writes=[bt1[sl]])
                        S.op("act", lambda e, sl=sl, d=d, ft=ft, lo=lo, hi=hi, n2=n2: e.activation(
                            out=AA[:, lo:hi], in_=T1[:, sl, 0:n2], func=AF.Exp, scale=KAP[:, d * 2 + ft:d * 2 + ft + 1]),
                            reads=[bt1[sl], b_kap], writes=[b_aa])
                        S.op("act", lambda e, sl=sl, bi_=bi_, n2=n2, ci=ci: e.activation(
                            out=T2[:, sl, 0:n2], in_=ps[:, bi_, 0:n2], func=AF.Sigmoid, bias=SM[:, ci:ci + 1], scale=1.0),
                            reads=[bps[bi_], b_sm], writes=[bt2[sl]])
                        S.op("act", lambda e, sl=sl, lo=lo, hi=hi, n2=n2: e.activation(
                            out=T1[:, sl, 0:n2], in_=AA[:, lo:hi], func=AF.Square), reads=[b_aa], writes=[bt1[sl]])
                        S.op("dve", lambda e, sl=sl, n2=n2: e.tensor_scalar(
                            out=T1[:, sl, 0:n2], in0=T1[:, sl, 0:n2], scalar1=1.0, scalar2=-1.0, op0=ALU.min, op1=ALU.mult),
                            reads=[bt1[sl]], writes=[bt1[sl]])
                        S.op("act", lambda e, sl=sl, n2=n2: e.activation(
                            out=T1[:, sl, 0:n2], in_=T1[:, sl, 0:n2], func=AF.Sqrt, scale=1.0, bias=ONE1[:]), reads=[bt1[sl], b_kap], writes=[bt1[sl]])
                        S.op("dve", lambda e, sl=sl, lo=lo, hi=hi, n2=n2: e.tensor_tensor(
                            out=T2[:, sl, 0:n2], in0=T2[:, sl, 0:n2], in1=XC[:, lo:hi], op=ALU.mult), reads=[bt2[sl], b_xc], writes=[bt2[sl]])
                        S.op("dve", lambda e, sl=sl, lo=lo, hi=hi, n2=n2: e.tensor_tensor(
                            out=UU[:, lo:hi], in0=T2[:, sl, 0:n2], in1=T1[:, sl, 0:n2], op=ALU.mult), reads=[bt2[sl], bt1[sl]], writes=[b_uu])
                    dst, bd = (HS, b_hs) if d == 0 else (HB, b_hb)
                    cs, ls = slice(CT0, CT0 + 256), slice(LT0, LT0 + 4096)
                    if d == 0:
                        S.op("dve", lambda e, dst=dst, cs=cs: e.tensor_tensor_scan(
                            out=dst[:, cs], data0=AA[:, cs], data1=UU[:, cs], initial=0.0, op0=ALU.mult, op1=ALU.add),
                            reads=[b_aa, b_uu], writes=[bd])
                        S.op("dve", lambda e, dst=dst, ls=ls: e.tensor_tensor_scan(
                            out=dst[:, ls], data0=AA[:, ls], data1=UU[:, ls], initial=dst[:, CT0 + 255:CT0 + 256], op0=ALU.mult, op1=ALU.add),
                            reads=[b_aa, b_uu, bd], writes=[bd])
                    else:
                        S.op("dve", lambda e, dst=dst, cs=cs: e.tensor_tensor_scan(
                            out=rev(dst[:, cs]), data0=rev(AA[:, cs]), data1=rev(UU[:, cs]), initial=0.0, op0=ALU.mult, op1=ALU.add),
                            reads=[b_aa, b_uu], writes=[bd])
                        S.op("dve", lambda e, dst=dst, ls=ls: e.tensor_tensor_scan(
                            out=rev(dst[:, ls]), data0=rev(AA[:, ls]), data1=rev(UU[:, ls]), initial=dst[:, CT0:CT0 + 1], op0=ALU.mult, op1=ALU.add),
                            reads=[b_aa, b_uu, bd], writes=[bd])
                for rg_ in (slice(CT0, CT0 + 256), slice(LT0, LT0 + 4096)):
                    S.op("pool", lambda e, rg_=rg_: e.tensor_tensor(out=HS[:, rg_], in0=HS[:, rg_], in1=HB[:, rg_], op=ALU.add),
                         reads=[b_hs, b_hb], writes=[b_hs])
                S.op("dve", lambda e, ft=ft: e.tensor_tensor(out=YR[:, 0:256], in0=HS[:, CT0:CT0 + 256], in1=ZG[:, ft, 0:256], op=ALU.mult),
                     reads=[b_hs, b_zg[ft]], writes=[b_yr])
                S.op("dve", lambda e, ft=ft: e.tensor_tensor(
                    out=YR[:, 256:T].rearrange("p (w r) -> p r w", r=64),
                    in0=HS[:, LT0:LT0 + 4096].rearrange("p (r w) -> p r w", w=64),
                    in1=ZG[:, ft, 256:T].rearrange("p (r w) -> p r w", w=64), op=ALU.mult),
                    reads=[b_hs, b_zg[ft]], writes=[b_yr])
                S.dma("sp", yx_out[256 + ft * 128:256 + (ft + 1) * 128, :], YR[:], reads=[b_yr])
            S.flush()


class StopStage(Exception):
    pass


import os
DBG = set(os.environ.get("K_DBG", "").split(","))


def build_b(stop=None):
    nc = new_nc()
    S = Sched(nc)
    S.stop = stop
    S.nflush = 0
    hx_all = din(nc, "hx_all", [4, 2048, OWN], BF16)
    wM = din(nc, "wM", [16, 128, 1028])
    wR = din(nc, "wR", [16, 128, 512])
    sm = din(nc, "sm", [128, NSM])
    cst = din(nc, "cst", [128, 6, 128])
    wrg = din(nc, "wrg", [128, 8, 128])
    yx = dout(nc, "yx", [512, T], BF16)
    try:
        stage_b(nc, S, hx_all, wM, wR, sm, cst, wrg, yx)
    except StopStage:
        pass
    return nc


def make_cst():
    k = np.arange(128)[:, None]
    m = np.arange(128)[None, :]
    U = (k <= m).astype(np.float32)
    Lm = (k >= m).astype(np.float32)
    same = ((k // 64) == (m // 64)).astype(np.float32)
    ident = np.eye(128, dtype=np.float32)
    ones = np.ones((128, 128), np.float32)
    return np.ascontiguousarray(np.stack([U, Lm, U * same, Lm * same, ident, ones], axis=1))


def kc_major(w):
    return np.ascontiguousarray(w.reshape(16, 128, -1))


def b_inputs(inp, l, h):
    w_in = inp["w_in"][l]
    cols = np.concatenate([
        np.arange(h * 128, (h + 1) * 128), 512 + np.arange(h * 128, (h + 1) * 128),
        1024 + np.arange(h * 256, (h + 1) * 256), 2048 + np.arange(h * 256, (h + 1) * 256),
        3072 + np.arange(h * 256, (h + 1) * 256), 4096 + np.array([h, 4 + h, 8 + h, 12 + h])])
    wM = kc_major(w_in[:, cols])
    colsR = np.concatenate([4112 + np.arange(h * 256, (h + 1) * 256), 5136 + np.arange(h * 256, (h + 1) * 256)])
    wR = kc_major(w_in[:, colsR])
    sm = np.zeros((128, NSM), np.float32)
    cw, cb = inp["conv_qk_w"][l], inp["conv_qk_b"][l]
    qs = slice(h * 128, (h + 1) * 128)
    ks = slice(512 + h * 128, 512 + (h + 1) * 128)
    sm[:, 0:4] = cw[:, qs].T
    sm[:, 4] = cb[qs]
    sm[:, 5:9] = cw[:, ks].T
    sm[:, 9] = cb[ks]
    rw, rb = inp["conv_r_w"][l], inp["conv_r_b"][l]
    for ft in range(2):
        cs = slice(h * 256 + ft * 128, h * 256 + (ft + 1) * 128)
        sm[:, 10 + 5 * ft:14 + 5 * ft] = rw[:, cs].T
        sm[:, 14 + 5 * ft] = rb[cs]
        for d in range(2):
            sm[:, 32 + d * 2 + ft] = inp["lru_lambda"][l, d, cs]
            for g in range(2):
                sm[:, 24 + d * 4 + g * 2 + ft] = inp["b_rg"][l, d, g, cs]
    sm[:, 20:24] = inp["b_gate"][l][[h, 4 + h, 8 + h, 12 + h]][None, :]
    sm[:, 36:292] = inp["m_norm_g"][l][h * 256:(h + 1) * 256][None, :]
    wrg = np.zeros((128, 8, 128), np.float32)
    for d in range(2):
        for g in range(2):
            for ft in range(2):
                for bb in range(2):
                    blk = h * 4 + ft * 2 + bb
                    wrg[bb * 64:(bb + 1) * 64, d * 4 + g * 2 + ft, bb * 64:(bb + 1) * 64] = inp["w_rg"][l, d, g, blk]
    return {"wM": wM, "wR": wR, "sm": sm, "wrg": wrg, "cst": make_cst()}


def host_b(inp, l, hx_parts):
    maps = []
    for core in range(NCORES):
        b, h = core // 4, core % 4
        m = b_inputs(inp, l, h)
        m["hx_all"] = np.ascontiguousarray(np.stack(hx_parts[4 * b:4 * b + 4], axis=0))
        maps.append(m)
    res = run(prog("b", build_b), maps)
    return [r["yx"] for r in res]


def wout_perm(w_out_l):
    rows = []
    for h in range(4):
        for i in range(4):
            base = (h * 256 + i * 128) if i < 2 else (1024 + h * 256 + (i - 2) * 128)
            rows.append(np.arange(base, base + 128))
    return np.ascontiguousarray(w_out_l[np.concatenate(rows)])


def yx_for(yx_parts, b, j):
    out = np.empty((16, 128, OWN), dtype=yx_parts[0].dtype)
    for h in range(4):
        y = yx_parts[4 * b + h]
        for i in range(4):
            out[h * 4 + i, :, 0:64] = y[i * 128:(i + 1) * 128, 64 * j:64 * j + 64]
            out[h * 4 + i, :, 64:] = y[i * 128:(i + 1) * 128, 256 + 1024 * j:256 + 1024 * (j + 1)]
    return out


def host_ca(inp, modsAll, l, xs, yx_parts, final):
    maps = []
    wp = wout_perm(inp["w_out"][l])
    for core in range(NCORES):
        b, j = core // 4, core % 4
        m = {"x_in": xs[core], "yx": yx_for(yx_parts, b, j), "wout": wp,
             "mods": mods_for(modsAll, b, l, None if final else l + 1),
             "ng": vec_fm(inp["final_g"] if final else inp["norm_g"][l + 1])}
        maps.append(m)
    res = run(prog("cf" if final else "ca", lambda: build_ca(True, final)), maps)
    if final:
        return [r["out"] for r in res]
    return [r["x_out"] for r in res], [r["hx"] for r in res]


def kernel(**inp):
    inp = {k: np.asarray(v) for k, v in inp.items()}
    modsAll = host_k0(inp)
    maps, hx = host_a0(inp, modsAll)
    xs = [m["x_in"] for m in maps]
    for l in range(L):
        yx = host_b(inp, l, hx)
        if l < L - 1:
            xs, hx = host_ca(inp, modsAll, l, xs, yx, False)
        else:
            outs = host_ca(inp, modsAll, l, xs, yx, True)
    out = np.empty((2, SEQ, D), np.float32)
    for core in range(NCORES):
        b, j = core // 4, core % 4
        o = outs[core].T.reshape(16, 64, D)
        out[b].reshape(64, 64, D)[:, 16 * j:16 * j + 16, :] = o.transpose(1, 0, 2)
    return out
```
